# Optimizing a Trainium2 kernel written in Bass

```python
import math
import jax, jax.numpy as jnp
from jax import lax
import numpy as np

D_MODEL = 1024
BATCH = 2
SEQ = 8192
DEPTH = 4

N_A_LAYERS = DEPTH // 2
N_B_LAYERS = DEPTH - N_A_LAYERS
CONV_WIDTH = 31
N_HEADS = 16
HEAD_DIM = D_MODEL // N_HEADS
D_FF = 4 * D_MODEL
D_PLE = 256
Q_BLOCK = 128
EPS = 1e-6
NEG_BIG = -1e30

kernel_name = "yoco_conformer_fox_hybrid"


def rmsnorm(x, g):
    xf = x.astype(jnp.float32)
    y = xf * lax.rsqrt(jnp.mean(xf * xf, axis=-1, keepdims=True) + EPS)
    return (y * g.astype(jnp.float32)).astype(x.dtype)


def layernorm(x, g, b):
    xf = x.astype(jnp.float32)
    mu = jnp.mean(xf, axis=-1, keepdims=True)
    xc = xf - mu
    var = jnp.mean(xc * xc, axis=-1, keepdims=True)
    y = xc * lax.rsqrt(var + EPS)
    return (y * g.astype(jnp.float32) + b.astype(jnp.float32)).astype(x.dtype)


def conformer_conv(hn, w_pw1, b_pw1, w_dw, b_dw, ln_g, ln_b, w_pw2, b_pw2):
    u = hn @ w_pw1 + b_pw1
    a, g = jnp.split(u, 2, axis=-1)
    u = a * jax.nn.sigmoid(g)
    kern = w_dw[:, None, :].astype(u.dtype)
    u = lax.conv_general_dilated(
        u, kern, window_strides=(1,), padding=((CONV_WIDTH - 1, 0),),
        dimension_numbers=("NWC", "WIO", "NWC"),
        feature_group_count=D_MODEL) + b_dw
    u = layernorm(u, ln_g, ln_b)
    u = jax.nn.silu(u)
    return u @ w_pw2 + b_pw2


def shared_kv(h, kv_norm, w_kvf, b_f):
    B, S, _ = h.shape
    u = rmsnorm(h, kv_norm) @ w_kvf
    k = u[..., :D_MODEL].reshape(B, S, N_HEADS, HEAD_DIM)
    v = u[..., D_MODEL:2 * D_MODEL].reshape(B, S, N_HEADS, HEAD_DIM)
    f_logit = (u[..., 2 * D_MODEL:] + b_f).astype(jnp.float32)
    log_f = jax.nn.log_sigmoid(f_logit)
    c = jnp.cumsum(log_f, axis=1)
    return k, v, jnp.transpose(c, (0, 2, 1))


def fox_attention(hn, w_q, w_o, k, v, c_bhs):
    B, S, _ = hn.shape
    nb = S // Q_BLOCK
    q = (hn @ w_q).reshape(B, S, N_HEADS, HEAD_DIM) * (HEAD_DIM ** -0.5)
    qb = jnp.transpose(q.reshape(B, nb, Q_BLOCK, N_HEADS, HEAD_DIM), (1, 0, 2, 3, 4))
    cb = jnp.transpose(c_bhs.reshape(B, N_HEADS, nb, Q_BLOCK), (2, 0, 1, 3))
    k_pos = jnp.arange(S)

    def one_block(args):
        q_blk, c_blk, i = args
        s = jnp.einsum("bqhd,bkhd->bhqk", q_blk, k,
                       preferred_element_type=jnp.float32)
        bias = c_blk[:, :, :, None] - c_bhs[:, :, None, :]
        q_pos = i * Q_BLOCK + jnp.arange(Q_BLOCK)
        causal = k_pos[None, :] <= q_pos[:, None]
        s = jnp.where(causal, s + bias, NEG_BIG)
        pr = jax.nn.softmax(s, axis=-1)
        return jnp.einsum("bhqk,bkhd->bqhd", pr.astype(v.dtype), v)

    o = lax.map(one_block, (qb, cb, jnp.arange(nb)))
    o = jnp.transpose(o, (1, 0, 2, 3, 4)).reshape(B, S, D_MODEL)
    return o @ w_o


def setup_inputs(seed: int = 0) -> dict:
    key = jax.random.key(seed)
    ks = jax.random.split(key, 32)
    f32 = jnp.float32
    nrm = lambda k, shape, scale: (jax.random.normal(k, shape, f32) * scale)
    gain = lambda k, shape: 1.0 + 0.05 * jax.random.normal(k, shape, f32)
    D = D_MODEL
    x = jax.random.normal(ks[0], (BATCH, SEQ, D), f32)
    p = jax.random.normal(ks[1], (DEPTH, BATCH, SEQ, D_PLE), f32)
    mix_norm = gain(ks[2], (DEPTH, D))
    conv_w_pw1 = nrm(ks[3], (N_A_LAYERS, D, 2 * D), D ** -0.5)
    conv_b_pw1 = nrm(ks[4], (N_A_LAYERS, 2 * D), 0.02)
    conv_w_dw = nrm(ks[5], (N_A_LAYERS, CONV_WIDTH, D), CONV_WIDTH ** -0.5)
    conv_b_dw = nrm(ks[6], (N_A_LAYERS, D), 0.02)
    conv_ln_g = gain(ks[7], (N_A_LAYERS, D))
    conv_ln_b = nrm(ks[8], (N_A_LAYERS, D), 0.02)
    conv_w_pw2 = nrm(ks[9], (N_A_LAYERS, D, D), 0.5 * D ** -0.5)
    conv_b_pw2 = nrm(ks[10], (N_A_LAYERS, D), 0.02)
    kv_norm = gain(ks[11], (D,))
    w_kvf = jnp.concatenate([
        nrm(ks[12], (D, 2 * D), D ** -0.5),
        nrm(ks[13], (D, N_HEADS), 0.1 * D ** -0.5),
    ], axis=1)
    b_f = jax.random.uniform(ks[14], (N_HEADS,), f32, 1.0, 6.0)
    attn_w_q = nrm(ks[15], (N_B_LAYERS, D, D), D ** -0.5)
    attn_w_o = nrm(ks[16], (N_B_LAYERS, D, D), 0.5 * D ** -0.5)
    ffn_norm = gain(ks[17], (DEPTH, D))
    ffn_w1 = nrm(ks[18], (DEPTH, D, D_FF), D ** -0.5)
    ffn_w2 = nrm(ks[19], (DEPTH, D_FF, D), 0.5 * D_FF ** -0.5)
    ple_norm = gain(ks[20], (DEPTH, D))
    ple_w_gate = nrm(ks[21], (DEPTH, D, D), D ** -0.5)
    ple_w_proj = nrm(ks[22], (DEPTH, D_PLE, D), 0.5 * D_PLE ** -0.5)
    final_norm = gain(ks[23], (D,))
    return {"x": x, "p": p, "mix_norm": mix_norm,
            "conv_w_pw1": conv_w_pw1, "conv_b_pw1": conv_b_pw1,
            "conv_w_dw": conv_w_dw, "conv_b_dw": conv_b_dw,
            "conv_ln_g": conv_ln_g, "conv_ln_b": conv_ln_b,
            "conv_w_pw2": conv_w_pw2, "conv_b_pw2": conv_b_pw2,
            "kv_norm": kv_norm, "w_kvf": w_kvf, "b_f": b_f,
            "attn_w_q": attn_w_q, "attn_w_o": attn_w_o,
            "ffn_norm": ffn_norm, "ffn_w1": ffn_w1, "ffn_w2": ffn_w2,
            "ple_norm": ple_norm, "ple_w_gate": ple_w_gate, "ple_w_proj": ple_w_proj,
            "final_norm": final_norm}


def reference(x, p, mix_norm, conv_w_pw1, conv_b_pw1, conv_w_dw, conv_b_dw,
              conv_ln_g, conv_ln_b, conv_w_pw2, conv_b_pw2, kv_norm, w_kvf, b_f,
              attn_w_q, attn_w_o, ffn_norm, ffn_w1, ffn_w2, ple_norm, ple_w_gate,
              ple_w_proj, final_norm):
    h = x
    k = v = c_bhs = None
    for i in range(DEPTH):
        hn = rmsnorm(h, mix_norm[i])
        if i < N_A_LAYERS:
            h = h + conformer_conv(hn, conv_w_pw1[i], conv_b_pw1[i], conv_w_dw[i],
                                   conv_b_dw[i], conv_ln_g[i], conv_ln_b[i],
                                   conv_w_pw2[i], conv_b_pw2[i])
        else:
            j = i - N_A_LAYERS
            if j == 0:
                k, v, c_bhs = shared_kv(h, kv_norm, w_kvf, b_f)
            h = h + fox_attention(hn, attn_w_q[j], attn_w_o[j], k, v, c_bhs)
        hn = rmsnorm(h, ffn_norm[i])
        h = h + jnp.square(jax.nn.relu(hn @ ffn_w1[i])) @ ffn_w2[i]
        gate = jax.nn.sigmoid(rmsnorm(h, ple_norm[i]) @ ple_w_gate[i])
        h = h + gate * (p[i] @ ple_w_proj[i])
    return rmsnorm(h, final_norm)
```

```python
import numpy as np
import ml_dtypes
from contextlib import ExitStack
import concourse.bass as bass
import concourse.mybir as mybir
from concourse.bass_utils import run_bass_kernel_spmd

F32 = mybir.dt.float32
BF16 = mybir.dt.bfloat16
AF = mybir.ActivationFunctionType
ALU = mybir.AluOpType

D = 1024
NCH = 8
B = 2
S = 8192
DEPTH = 4
NH = 16
HD = 64
DFF = 4096
DPLE = 256
CW = 31
EPS = 1e-6
NCORES = 8
T = 2048
HL = 64
C = HL + T
UPAD = 32
HPC = 4
NV = 26

ENG = ("pe", "act", "dve", "pool", "sp")
DMA_K = 8


class _Op:
    __slots__ = ("eng", "fn", "deps", "pos", "dma", "dma_n", "sig", "semval", "waits", "name")


class Prog:
    def __init__(self):
        self.ops = []
        self.by_eng = {e: [] for e in ENG}
        self.last_w = {}
        self.readers = {}
        self.ndma = {e: 0 for e in ENG}
        self.dma_ops = {e: [] for e in ENG}

    def add(self, eng, fn, r=(), w=(), dma=False, name=""):
        deps = set()
        for k in r:
            if k in self.last_w:
                deps.add(self.last_w[k])
        for k in w:
            if k in self.last_w:
                deps.add(self.last_w[k])
            deps.update(self.readers.get(k, ()))
        op = _Op()
        op.eng, op.fn, op.dma, op.name = eng, fn, dma, name
        op.sig = False
        op.semval = None
        oid = len(self.ops)
        op.pos = len(self.by_eng[eng])
        op.dma_n = None
        if dma:
            op.dma_n = self.ndma[eng]
            self.ndma[eng] += 1
            if op.dma_n >= DMA_K:
                deps.add(self.dma_ops[eng][op.dma_n - DMA_K])
            self.dma_ops[eng].append(oid)
        op.deps = deps
        self.ops.append(op)
        self.by_eng[eng].append(oid)
        for k in r:
            self.readers.setdefault(k, []).append(oid)
        for k in w:
            self.last_w[k] = oid
            self.readers[k] = []
        return oid

    def _analyze(self):
        ops = self.ops
        for eng in ENG:
            waited_pos = {e: -1 for e in ENG}
            waited_dma = set()
            for oid in self.by_eng[eng]:
                op = ops[oid]
                final = []
                best = {}
                for d in sorted(op.deps):
                    P = ops[d]
                    if P.dma:
                        if d in waited_dma:
                            continue
                        waited_dma.add(d)
                        final.append(d)
                    else:
                        if P.eng == "pe" and eng == "pe":
                            continue
                        if waited_pos[P.eng] >= P.pos:
                            continue
                        waited_pos[P.eng] = P.pos
                        best[P.eng] = d
                final.extend(best.values())
                op.waits = final
                for d in final:
                    ops[d].sig = True
        for eng in ENG:
            c = 0
            for oid in self.by_eng[eng]:
                op = ops[oid]
                if op.dma:
                    op.semval = 16 * (op.dma_n // DMA_K + 1)
                elif op.sig:
                    c += 1
                    op.semval = c

    def emit(self, nc, final_wait=()):
        fid = self.add("sp", None, name="final")
        self.ops[fid].deps = set(final_wait)
        self._analyze()
        ops = self.ops
        with ExitStack() as st:
            esem = {e: st.enter_context(nc.semaphore("s_" + e)) for e in ENG}
            dsem = {e: [st.enter_context(nc.semaphore("d_%s%d" % (e, i))) for i in range(DMA_K)]
                    for e in ENG if self.ndma[e] > 0}
            block = st.enter_context(nc.Block())

            def run(eng, h):
                for oid in self.by_eng[eng]:
                    op = ops[oid]
                    for d in op.waits:
                        Pp = ops[d]
                        if Pp.dma:
                            h.wait_ge(dsem[Pp.eng][Pp.dma_n % DMA_K], Pp.semval)
                        else:
                            h.wait_ge(esem[Pp.eng], Pp.semval)
                    if op.fn is None:
                        continue
                    ins = op.fn(h)
                    if op.dma:
                        ins.then_inc(dsem[eng][op.dma_n % DMA_K], 16)
                    elif op.sig:
                        ins.then_inc(esem[eng], 1)

            @block.tensor
            def _(h):
                run("pe", h)

            @block.scalar
            def _(h):
                run("act", h)

            @block.vector
            def _(h):
                run("dve", h)

            @block.gpsimd
            def _(h):
                run("pool", h)

            @block.sync
            def _(h):
                run("sp", h)


VEC_MIX, VEC_FFN, VEC_PLE, VEC_KV, VEC_FIN, VEC_CONV = 0, 4, 8, 12, 13, 14
SLOT = 4096


class KB:
    def __init__(self, nc, st):
        self.nc = nc
        self.st = st
        self.P = Prog()
        self.psum = [st.enter_context(nc.psum_tensor("ps%d" % i, [128, 512], F32)) for i in range(8)]
        self.psi = 0
        self.nrot = 8
        self.out_dmas = []

    def sb(self, name, shape, dt):
        return self.st.enter_context(self.nc.sbuf_tensor(name, shape, dt))

    def ps(self):
        i = self.psi
        self.psi = (self.psi + 1) % self.nrot
        return self.psum[i], ("ps", i)

    def add(self, *a, **kw):
        return self.P.add(*a, **kw)

    def tp(self, nchunk=NCH, width=512):
        i = self.tpi
        self.tpi = (self.tpi + 1) % len(self.tpool)
        t = self.tpool[i][:, 0:nchunk * width].rearrange("p (c n) -> p c n", c=nchunk)
        return t, [("tp", i, c) for c in range(nchunk)]

    def tf(self):
        i = self.tfi
        self.tfi = (self.tfi + 1) % len(self.tmpf)
        return self.tmpf[i], ("tmpf", i)


def tiles_of(c0, c1, step=512):
    out = []
    c = c0
    while c < c1:
        n = min(step, c1 - c)
        out.append((c, n))
        c += n
    return out


class WM:
    def __init__(self, k, w2d, r0, nrows, c0, ncols):
        self.kcs = nrows // 128
        self.pc = min(ncols, SLOT // self.kcs)
        self.pieces = []
        for j in range(0, ncols, self.pc):
            i = k.wsi
            k.wsi = (k.wsi + 1) % len(k.wslots)
            view = k.wslots[i][:, 0:self.kcs * self.pc].rearrange("p (c n) -> p c n", c=self.kcs)
            src = w2d[r0:r0 + nrows, c0 + j:c0 + j + self.pc].rearrange("(c p) n -> p c n", p=128)
            k.add("pool", lambda h, view=view, src=src: h.dma_start(out=view, in_=src, max_dma_last_dim=4096),
                  w=[("wslot", i)], dma=True)
            self.pieces.append((view, ("wslot", i)))

    def lhsT(self, kc, m):
        v, _ = self.pieces[(m * 128) // self.pc]
        o = (m * 128) % self.pc
        return v[:, kc, o:o + 128]

    def cols(self, kc, c0, n):
        v, _ = self.pieces[c0 // self.pc]
        o = c0 % self.pc
        return v[:, kc, o:o + n]

    def key(self, m):
        return self.pieces[(m * 128) // self.pc][1]

    def keyc(self, c0):
        return self.pieces[c0 // self.pc][1]


def build_program(seg, layer=None):
    nc = bass.Bass("TRN2", target_bir_lowering=False)
    with ExitStack() as st:
        k = KB(nc, st)
        if seg == "A":
            seg_A(k)
        elif seg == "ATT":
            seg_ATT(k, layer)
        else:
            seg_POST(k, layer)
        k.P.emit(nc, final_wait=k.out_dmas)
    return nc


def setup_common(k, nslots=5, ntp=5):
    nc = k.nc
    k.vecs_d = nc.dram_tensor("vecs", [128, NCH, NV], F32, kind="ExternalInput").ap()
    k.vecs = k.sb("vecs_sb", [128, NCH, NV], F32)
    k.add("sp", lambda h: h.dma_start(out=k.vecs[:, :, :], in_=k.vecs_d[:, :, :]), w=["vecs"], dma=True)
    k.ones = k.sb("ones_bf", [128, 128], BF16)
    k.add("dve", lambda h: h.memset(k.ones[:, :], 1.0), w=["ones"])
    k.epsb = k.sb("epsb", [128, 1], F32)
    k.add("dve", lambda h: h.memset(k.epsb[:, :], EPS), w=["epsb"])
    k.wslots = [k.sb("wslot%d" % i, [128, SLOT], BF16) for i in range(nslots)]
    k.wsi = 0
    k.tpool = [k.sb("tp%d" % i, [128, SLOT], BF16) for i in range(ntp)]
    k.tpi = 0
    k.tmpf = [k.sb("tmpf%d" % i, [128, 512], F32) for i in range(3)]
    k.tfi = 0
    k.rs = k.sb("rs", [128, 512], F32)
    k.rinv = k.sb("rinv", [128, 512], F32)


def vec(k, idx, c):
    return k.vecs[:, c, idx:idx + 1]


def mm_acc(pt, n, pairs, m=128):
    def f(h):
        ins = None
        L = len(pairs)
        for i, (a, b) in enumerate(pairs):
            ins = h.matmul(pt[0:m, 0:n], lhsT=a, rhs=b, start=(i == 0), stop=(i == L - 1))
        return ins
    return f


def rms_stats(k, src, src_keys, c0, n):
    sq, sqk = k.tp()
    k.add("act", lambda h: h.activation(out=sq[:, :, 0:n], in_=src[:, :, c0:c0 + n], func=AF.Square),
          r=src_keys, w=sqk)
    pt, pk = k.ps()
    k.add("pe", mm_acc(pt, n, [(k.ones[:, :], sq[:, c, 0:n]) for c in range(NCH)]), r=sqk + ["ones"], w=[pk])
    k.add("act", lambda h: h.activation(out=k.rs[:, 0:n], in_=pt[:, 0:n], func=AF.Sqrt, bias=k.epsb[:, 0:1], scale=1.0 / D),
          r=[pk, "epsb"], w=["rs"])
    k.add("dve", lambda h: h.reciprocal(out=k.rinv[:, 0:n], in_=k.rs[:, 0:n]), r=["rs"], w=["rinv"])


def rmsnorm_tile(k, src, src_keys, c0, n, gidx, dst, dst_c0, dst_keys):
    rms_stats(k, src, src_keys, c0, n)
    for c in range(NCH):
        k.add("dve", lambda h, c=c: h.scalar_tensor_tensor(
            out=dst[:, c, dst_c0:dst_c0 + n], in0=src[:, c, c0:c0 + n], scalar=vec(k, gidx, c),
            in1=k.rinv[:, 0:n], op0=ALU.mult, op1=ALU.mult),
            r=[src_keys[c], "rinv", "vecs"], w=[dst_keys[c]])


def linear_fm(k, wm, xin, xkeys, xc0, n, epilogue, nm=NCH, kcs=NCH):
    for m in range(nm):
        pt, pk = k.ps()
        k.add("pe", mm_acc(pt, n, [(wm.lhsT(kc, m), xin[:, kc, xc0:xc0 + n]) for kc in range(kcs)]),
              r=list(xkeys) + [wm.key(m)], w=[pk])
        epilogue(m, pt, pk)


def decl_weights(k, names):
    nc = k.nc
    shapes = dict(conv_w_pw1=[2, D, 2 * D], conv_w_pw2=[2, D, D], ffn_w1=[DEPTH, D, DFF], ffn_w2=[DEPTH, DFF, D],
                  ple_w_gate=[DEPTH, D, D], ple_w_proj=[DEPTH, DPLE, D], w_kvf=[D, 2 * D + NH],
                  attn_w_q=[2, D, D], attn_w_o=[2, D, D])
    return {nm: nc.dram_tensor(nm, shapes[nm], F32, kind="ExternalInput").ap() for nm in names}


def seg_A(k):
    nc = k.nc
    xT = nc.dram_tensor("xT", [128, NCH, C], F32, kind="ExternalInput").ap()
    pT = nc.dram_tensor("pT", [2, 128, 2, C], F32, kind="ExternalInput").ap()
    hmask_d = nc.dram_tensor("hmask", [128, 1], F32, kind="ExternalInput").ap()
    wdw_d = nc.dram_tensor("wdw", [2, 128, NCH, CW], F32, kind="ExternalInput").ap()
    bf_d = nc.dram_tensor("bf", [NH, 1], F32, kind="ExternalInput").ap()
    W = decl_weights(k, ["conv_w_pw1", "conv_w_pw2", "ffn_w1", "ffn_w2", "ple_w_gate", "ple_w_proj", "w_kvf", "attn_w_q"])
    hT_o = nc.dram_tensor("hT_out", [128, NCH, T], F32, kind="ExternalOutput").ap()
    kT_o = nc.dram_tensor("kT_out", [128, NCH, T], BF16, kind="ExternalOutput").ap()
    v_o = nc.dram_tensor("v_out", [128, T // 128, D], BF16, kind="ExternalOutput").ap()
    lf_o = nc.dram_tensor("lf_out", [NH, T], F32, kind="ExternalOutput").ap()
    qT_o = nc.dram_tensor("qT_out", [128, NCH, T], BF16, kind="ExternalOutput").ap()

    setup_common(k, nslots=5, ntp=4)
    k.h = k.sb("h", [128, NCH, C], F32)
    NT5 = tiles_of(0, HL, 64) + tiles_of(HL, C)
    NT4 = tiles_of(HL, C)
    k.tile_id = {c0: i for i, (c0, n) in enumerate(NT5)}

    def hkeys(c0):
        return [("h", k.tile_id[c0], c) for c in range(NCH)]
    k.hkeys = hkeys
    for (c0, n) in NT5:
        k.add("sp", lambda h, c0=c0, n=n: h.dma_start(out=k.h[:, :, c0:c0 + n], in_=xT[:, :, c0:c0 + n]),
              w=hkeys(c0), dma=True)
    k.hmask = k.sb("hmask_sb", [128, 1], F32)
    k.add("sp", lambda h: h.dma_start(out=k.hmask[:, :], in_=hmask_d[:, :]), w=["hmask"], dma=True)
    k.wdw = k.sb("wdw_sb", [128, NCH, CW], F32)
    k.ident = k.sb("ident", [128, 128], BF16)
    k.add("pool", lambda h: h.memset(k.ident[:, :], 0.0), w=["ident"])
    k.add("pool", lambda h: h.affine_select(out=k.ident[:, :], in_=k.ident[:, :], pattern=[[-1, 128]],
                                            compare_op=ALU.not_equal, fill=1.0, base=0, channel_multiplier=1),
          r=["ident"], w=["ident"])
    k.big = k.sb("big", [128, NCH, UPAD + C], BF16)
    k.dg = [k.sb("dg%d" % i, [128, CW, 128], BF16) for i in range(2)]
    k.pTb = [k.sb("pTb%d" % i, [128, 2, 512], BF16) for i in range(2)]
    k.pti = 0

    def bigkeys(c0):
        return [("big", k.tile_id[c0], c) for c in range(NCH)]
    k.bigkeys = bigkeys

    def conv_layer(li):
        cols_in = NT5
        cols_out = NT5 if li == 0 else NT4
        vb = VEC_CONV + 6 * li
        k.add("sp", lambda h, li=li: h.dma_start(out=k.wdw[:, :, :], in_=wdw_d[li]), w=["wdw"], dma=True)
        k.add("dve", lambda h: h.memset(k.big[:, :, 0:UPAD], 0.0), w=[("big", "pad")])
        wa = WM(k, W["conv_w_pw1"][li], 0, D, 0, D)
        wg = WM(k, W["conv_w_pw1"][li], 0, D, D, D)
        for (c0, n) in cols_in:
            xn, xk = k.tp()
            rmsnorm_tile(k, k.h, hkeys(c0), c0, n, VEC_MIX + li, xn, 0, xk)
            for m in range(NCH):
                pa, pak = k.ps()
                pg, pgk = k.ps()
                k.add("pe", mm_acc(pa, n, [(wa.lhsT(kc, m), xn[:, kc, 0:n]) for kc in range(NCH)]), r=xk + [wa.key(m)], w=[pak])
                k.add("pe", mm_acc(pg, n, [(wg.lhsT(kc, m), xn[:, kc, 0:n]) for kc in range(NCH)]), r=xk + [wg.key(m)], w=[pgk])
                tf, tfk = k.tf()
                k.add("act", lambda h, pg=pg, tf=tf, m=m, n=n: h.activation(
                    out=tf[:, 0:n], in_=pg[:, 0:n], func=AF.Sigmoid, bias=vec(k, vb + 1, m), scale=1.0),
                    r=[pgk, "vecs"], w=[tfk])
                k.add("dve", lambda h, pa=pa, tf=tf, m=m, n=n, c0=c0: h.scalar_tensor_tensor(
                    out=k.big[:, m, UPAD + c0:UPAD + c0 + n], in0=pa[:, 0:n], scalar=vec(k, vb, m),
                    in1=tf[:, 0:n], op0=ALU.add, op1=ALU.mult),
                    r=[pak, tfk, "vecs"], w=[("big", k.tile_id[c0], m)])
        k.add("dve", lambda h: h.tensor_scalar(out=k.big[:, :, UPAD:UPAD + HL], in0=k.big[:, :, UPAD:UPAD + HL],
                                               scalar1=k.hmask[:, 0:1], scalar2=None, op0=ALU.mult),
              r=["hmask"] + bigkeys(0), w=bigkeys(0))
        w2 = WM(k, W["conv_w_pw2"][li], 0, D, 0, D)
        allbig = [("big", ti, c) for ti in range(len(NT5)) for c in range(NCH)] + [("big", "pad")]
        dgi = 0
        for (c0, n) in cols_out:
            ti = k.tile_id[c0]
            cacc, ck = k.tp()
            for c in range(NCH):
                di = dgi
                dgi ^= 1
                dg = k.dg[di]

                def mk(h, c=c, dg=dg):
                    ins = None
                    for w_ in range(CW):
                        ins = h.tensor_scalar(out=dg[:, w_, :], in0=k.ident[:, :], scalar1=k.wdw[:, c, w_:w_ + 1],
                                              scalar2=None, op0=ALU.mult)
                    return ins
                k.add("dve", mk, r=["ident", "wdw"], w=[("dg", di)])
                pt, pk = k.ps()
                base = UPAD + c0 - (CW - 1)
                k.add("pe", mm_acc(pt, n, [(dg[:, w_, :], k.big[:, c, base + w_:base + w_ + n]) for w_ in range(CW)]),
                      r=[("big", t2, c) for t2 in range(len(NT5))] + [("big", "pad"), ("dg", di)], w=[pk])
                k.add("act", lambda h, pt=pt, c=c, n=n, cacc=cacc: h.activation(
                    out=cacc[:, c, 0:n], in_=pt[:, 0:n], func=AF.Identity, bias=vec(k, vb + 2, c), scale=1.0),
                    r=[pk, "vecs"], w=[ck[c]])
            sq, sqk = k.tp()
            k.add("act", lambda h, sq=sq, cacc=cacc, n=n: h.activation(out=sq[:, :, 0:n], in_=cacc[:, :, 0:n], func=AF.Square),
                  r=ck, w=sqk)
            p1, p1k = k.ps()
            p2, p2k = k.ps()
            k.add("pe", mm_acc(p1, n, [(k.ones[:, :], cacc[:, c, 0:n]) for c in range(NCH)]), r=ck + ["ones"], w=[p1k])
            k.add("pe", mm_acc(p2, n, [(k.ones[:, :], sq[:, c, 0:n]) for c in range(NCH)]), r=sqk + ["ones"], w=[p2k])
            mean, mk_ = k.tf()
            m2, m2k = k.tf()
            k.add("act", lambda h, p1=p1, mean=mean, n=n: h.activation(out=mean[:, 0:n], in_=p1[:, 0:n], func=AF.Identity, scale=1.0 / D),
                  r=[p1k], w=[mk_])
            k.add("dve", lambda h, mean=mean, m2=m2, n=n: h.tensor_tensor(out=m2[:, 0:n], in0=mean[:, 0:n], in1=mean[:, 0:n], op=ALU.mult),
                  r=[mk_], w=[m2k])
            k.add("dve", lambda h, p2=p2, m2=m2, n=n: h.scalar_tensor_tensor(
                out=m2[:, 0:n], in0=p2[:, 0:n], scalar=1.0 / D, in1=m2[:, 0:n], op0=ALU.mult, op1=ALU.subtract),
                r=[p2k, m2k], w=[m2k])
            k.add("act", lambda h, m2=m2, n=n: h.activation(out=k.rs[:, 0:n], in_=m2[:, 0:n], func=AF.Sqrt, bias=k.epsb[:, 0:1], scale=1.0),
                  r=[m2k, "epsb"], w=["rs"])
            k.add("dve", lambda h, n=n: h.reciprocal(out=k.rinv[:, 0:n], in_=k.rs[:, 0:n]), r=["rs"], w=["rinv"])
            xn, xk = k.tp()
            tf, tfk = k.tf()
            for c in range(NCH):
                k.add("dve", lambda h, tf=tf, c=c, n=n, mean=mean, cacc=cacc: h.tensor_tensor(
                    out=tf[:, 0:n], in0=cacc[:, c, 0:n], in1=mean[:, 0:n], op=ALU.subtract),
                    r=[ck[c], mk_], w=[tfk])
                k.add("dve", lambda h, tf=tf, n=n: h.tensor_tensor(
                    out=tf[:, 0:n], in0=tf[:, 0:n], in1=k.rinv[:, 0:n], op=ALU.mult),
                    r=[tfk, "rinv"], w=[tfk])
                k.add("act", lambda h, tf=tf, xn=xn, c=c, n=n: h.activation(
                    out=xn[:, c, 0:n], in_=tf[:, 0:n], func=AF.Silu, bias=vec(k, vb + 4, c), scale=vec(k, vb + 3, c)),
                    r=[tfk, "vecs"], w=[xk[c]])

            def ep(m, pt, pk, c0=c0, n=n, ti=ti):
                k.add("dve", lambda h: h.scalar_tensor_tensor(
                    out=k.h[:, m, c0:c0 + n], in0=pt[:, 0:n], scalar=vec(k, vb + 5, m), in1=k.h[:, m, c0:c0 + n],
                    op0=ALU.add, op1=ALU.add),
                    r=[pk, "vecs", ("h", ti, m)], w=[("h", ti, m)])
            linear_fm(k, w2, xn, xk, 0, n, ep)
        ffn_ple(k, li, cols_out, pT[li], W["ffn_w1"][li], W["ffn_w2"][li], W["ple_w_gate"][li], W["ple_w_proj"][li])

    for li in range(2):
        conv_layer(li)
    kv_proj(k, NT4, W["w_kvf"], bf_d, kT_o, v_o, lf_o)
    q_proj(k, NT4, VEC_MIX + 2, W["attn_w_q"][0], qT_o)
    for (c0, n) in NT4:
        k.out_dmas.append(k.add("sp", lambda h, c0=c0, n=n: h.dma_start(out=hT_o[:, :, c0 - HL:c0 - HL + n], in_=k.h[:, :, c0:c0 + n]),
                                r=hkeys(c0), dma=True))


def ffn_ple(k, li, cols, pT_l, w_f1, w_f2, w_pg, w_pp):
    xall = k.big
    XO = UPAD
    hkeys, bigkeys = k.hkeys, k.bigkeys
    for (c0, n) in cols:
        rmsnorm_tile(k, k.h, hkeys(c0), c0, n, VEC_FFN + li, xall, XO + c0, bigkeys(c0))
    for j in range(8):
        w1 = WM(k, w_f1, 0, D, j * 512, 512)
        w2 = WM(k, w_f2, j * 512, 512, 0, D)
        for (c0, n) in cols:
            ti = k.tile_id[c0]
            hid, hk = k.tp(4, 512)

            def ep1(m, pt, pk, hid=hid, hk=hk, n=n):
                k.add("act", lambda h: h.activation(out=hid[:, m, 0:n], in_=pt[:, 0:n], func=AF.Relu), r=[pk], w=[hk[m]])
                k.add("dve", lambda h: h.tensor_tensor(out=hid[:, m, 0:n], in0=hid[:, m, 0:n], in1=hid[:, m, 0:n], op=ALU.mult),
                      r=[hk[m]], w=[hk[m]])
            linear_fm(k, w1, xall, bigkeys(c0), XO + c0, n, ep1, nm=4)

            def ep2(m, pt, pk, c0=c0, n=n, ti=ti):
                k.add("dve", lambda h: h.tensor_tensor(out=k.h[:, m, c0:c0 + n], in0=pt[:, 0:n], in1=k.h[:, m, c0:c0 + n], op=ALU.add),
                      r=[pk, ("h", ti, m)], w=[("h", ti, m)])
            linear_fm(k, w2, hid, hk, 0, n, ep2, kcs=4)
    wg = WM(k, w_pg, 0, D, 0, D)
    wp = WM(k, w_pp, 0, DPLE, 0, D)
    for (c0, n) in cols:
        ti = k.tile_id[c0]
        pi = k.pti
        k.pti ^= 1
        pTb = k.pTb[pi]
        k.add("pool", lambda h, pTb=pTb, c0=c0, n=n: h.dma_start(out=pTb[:, :, 0:n], in_=pT_l[:, :, c0:c0 + n], max_dma_last_dim=4096),
              w=[("pTb", pi)], dma=True)
        xn, xk = k.tp()
        rmsnorm_tile(k, k.h, hkeys(c0), c0, n, VEC_PLE + li, xn, 0, xk)
        for m in range(NCH):
            pg, pgk = k.ps()
            pp, ppk = k.ps()
            k.add("pe", mm_acc(pg, n, [(wg.lhsT(kc, m), xn[:, kc, 0:n]) for kc in range(NCH)]), r=xk + [wg.key(m)], w=[pgk])
            k.add("pe", mm_acc(pp, n, [(wp.lhsT(kc, m), pTb[:, kc, 0:n]) for kc in range(2)]), r=[("pTb", pi), wp.key(m)], w=[ppk])
            tf, tfk = k.tf()
            k.add("act", lambda h, pg=pg, tf=tf, n=n: h.activation(out=tf[:, 0:n], in_=pg[:, 0:n], func=AF.Sigmoid), r=[pgk], w=[tfk])
            k.add("dve", lambda h, pp=pp, tf=tf, n=n: h.tensor_tensor(out=tf[:, 0:n], in0=pp[:, 0:n], in1=tf[:, 0:n], op=ALU.mult),
                  r=[ppk, tfk], w=[tfk])
            k.add("dve", lambda h, tf=tf, m=m, c0=c0, n=n: h.tensor_tensor(
                out=k.h[:, m, c0:c0 + n], in0=tf[:, 0:n], in1=k.h[:, m, c0:c0 + n], op=ALU.add),
                r=[tfk, ("h", ti, m)], w=[("h", ti, m)])


def q_proj(k, cols, gidx, w_q_l, qT_o):
    wq = WM(k, w_q_l, 0, D, 0, D)
    for (c0, n) in cols:
        xn, xk = k.tp()
        rmsnorm_tile(k, k.h, k.hkeys(c0), c0, n, gidx, xn, 0, xk)
        qt, qk = k.tp()

        def ep(m, pt, pk, qt=qt, qk=qk, n=n):
            k.add("act", lambda h: h.activation(out=qt[:, m, 0:n], in_=pt[:, 0:n], func=AF.Identity, scale=HD ** -0.5), r=[pk], w=[qk[m]])
        linear_fm(k, wq, xn, xk, 0, n, ep)
        k.out_dmas.append(k.add("sp", lambda h, qt=qt, c0=c0, n=n: h.dma_start(
            out=qT_o[:, :, c0 - HL:c0 - HL + n], in_=qt[:, :, 0:n]), r=qk, dma=True))


def kv_proj(k, cols, w_kvf, bf_d, kT_o, v_o, lf_o):
    wk_ = WM(k, w_kvf, 0, D, 0, D)
    wv_ = WM(k, w_kvf, 0, D, D, D)
    wf = k.sb("wf_sb", [128, NCH, NH], BF16)
    k.add("pool", lambda h: h.dma_start(out=wf[:, :, :], in_=w_kvf[:, 2 * D:2 * D + NH].rearrange("(c p) n -> p c n", p=128)),
          w=["wf"], dma=True)
    bfs = k.sb("bf_sb", [NH, 1], F32)
    k.add("sp", lambda h: h.dma_start(out=bfs[:, :], in_=bf_d[:, :]), w=["bfs"], dma=True)
    nbf = k.sb("nbf", [NH, 1], F32)
    k.add("dve", lambda h: h.tensor_scalar(out=nbf[:, :], in0=bfs[:, :], scalar1=-1.0, scalar2=None, op0=ALU.mult), r=["bfs"], w=["nbf"])
    oneb = k.sb("oneb", [128, 1], F32)
    k.add("dve", lambda h: h.memset(oneb[:, :], 1.0), w=["oneb"])
    for (c0, n) in cols:
        xn, xk = k.tp()
        rmsnorm_tile(k, k.h, k.hkeys(c0), c0, n, VEC_KV, xn, 0, xk)
        kt, kk = k.tp()

        def ep(m, pt, pk, kt=kt, kk=kk, n=n):
            k.add("act", lambda h: h.activation(out=kt[:, m, 0:n], in_=pt[:, 0:n], func=AF.Identity), r=[pk], w=[kk[m]])
        linear_fm(k, wk_, xn, xk, 0, n, ep)
        k.out_dmas.append(k.add("sp", lambda h, kt=kt, c0=c0, n=n: h.dma_start(
            out=kT_o[:, :, c0 - HL:c0 - HL + n], in_=kt[:, :, 0:n]), r=kk, dma=True))
        vt, vk = k.tp(4, 1024)
        for tb in range(n // 128):
            for half in range(2):
                pt, pk = k.ps()
                k.add("pe", mm_acc(pt, 512, [(xn[:, kc, tb * 128:(tb + 1) * 128], wv_.cols(kc, half * 512, 512)) for kc in range(NCH)]),
                      r=xk + [wv_.keyc(half * 512)], w=[pk])
                k.add("dve", lambda h, pt=pt, vt=vt, tb=tb, half=half: h.tensor_copy(out=vt[:, tb, half * 512:(half + 1) * 512], in_=pt[:, 0:512]),
                      r=[pk], w=[vk[tb]])
        blk0 = (c0 - HL) // 128
        k.out_dmas.append(k.add("sp", lambda h, vt=vt, blk0=blk0, n=n: h.dma_start(out=v_o[:, blk0:blk0 + n // 128, :], in_=vt[:, 0:n // 128, :]),
                                r=vk, dma=True))
        pt, pk = k.ps()
        k.add("pe", mm_acc(pt, n, [(wf[:, kc, :], xn[:, kc, 0:n]) for kc in range(NCH)], m=NH), r=xk + ["wf"], w=[pk])
        t1, t1k = k.tf()
        t2, t2k = k.tf()
        k.add("act", lambda h, pt=pt, t1=t1, n=n: h.activation(out=t1[0:NH, 0:n], in_=pt[0:NH, 0:n], func=AF.Exp, bias=nbf[:, 0:1], scale=-1.0),
              r=[pk, "nbf"], w=[t1k])
        k.add("act", lambda h, t1=t1, t2=t2, n=n: h.activation(out=t2[0:NH, 0:n], in_=t1[0:NH, 0:n], func=AF.Ln, bias=oneb[0:NH, 0:1], scale=1.0),
              r=[t1k, "oneb"], w=[t2k])
        k.add("dve", lambda h, t2=t2, n=n: h.tensor_scalar(out=t2[0:NH, 0:n], in0=t2[0:NH, 0:n], scalar1=-1.0, scalar2=None, op0=ALU.mult),
              r=[t2k], w=[t2k])
        k.out_dmas.append(k.add("sp", lambda h, t2=t2, c0=c0, n=n: h.dma_start(out=lf_o[:, c0 - HL:c0 - HL + n], in_=t2[0:NH, 0:n]),
                                r=[t2k], dma=True))


KA = HD + 6
NQT = S // 512
NKB = S // 128
MASKNEG = -30000.0


def seg_ATT(k, layer):
    nc = k.nc
    qTh = nc.dram_tensor("qTh", [HPC, HD, S], BF16, kind="ExternalInput").ap()
    kTh = nc.dram_tensor("kTh", [HPC, HD, S], BF16, kind="ExternalInput").ap()
    vh = nc.dram_tensor("vh", [HPC, 128, NKB, HD], BF16, kind="ExternalInput").ap()
    lf = nc.dram_tensor("lf", [HPC, S], F32, kind="ExternalInput").ap()
    oT_o = nc.dram_tensor("oT_out", [HPC, HD, S], BF16, kind="ExternalOutput").ap()
    caq = nc.dram_tensor("caq", [HPC, 3, S], BF16, kind="Internal").ap()
    cak = nc.dram_tensor("cak", [HPC, 3, S], BF16, kind="Internal").ap()
    k.nrot = 6
    ident = k.sb("identb", [128, 128], BF16)
    k.add("pool", lambda h: h.memset(ident[:, :], 0.0), w=["ident"])
    k.add("pool", lambda h: h.affine_select(out=ident[:, :], in_=ident[:, :], pattern=[[-1, 128]],
                                            compare_op=ALU.not_equal, fill=1.0, base=0, channel_multiplier=1),
          r=["ident"], w=["ident"])
    maskn = k.sb("maskn", [128, 128], BF16)
    k.add("pool", lambda h: h.memset(maskn[:, :], 0.0), w=["maskn"])
    k.add("pool", lambda h: h.affine_select(out=maskn[:, :], in_=maskn[:, :], pattern=[[1, 128]],
                                            compare_op=ALU.is_ge, fill=MASKNEG, base=0, channel_multiplier=-1),
          r=["maskn"], w=["maskn"])
    onesf = k.sb("onesf", [128, HD], F32)
    k.add("dve", lambda h: h.memset(onesf[:, :], 1.0), w=["onesf"])
    CS = 2048
    lfs = k.sb("lfs", [HPC, CS], F32)
    cc = k.sb("cc", [HPC, CS], F32)
    onesr = k.sb("onesr", [HPC, CS], F32)
    k.add("dve", lambda h: h.memset(onesr[:, :], 1.0), w=["onesr"])
    carry = k.sb("carry", [HPC, 1], F32)
    k.add("dve", lambda h: h.memset(carry[:, :], 0.0), w=["carry"])
    pcs = [k.sb("pc%d" % i, [HPC, CS], BF16) for i in range(3)]
    ncs = [k.sb("ncp%d" % i, [HPC, CS], BF16) for i in range(3)]
    rr = k.sb("rr", [HPC, CS], F32)
    sc_out = []
    for j in range(S // CS):
        sl = slice(j * CS, (j + 1) * CS)
        k.add("sp", lambda h, sl=sl: h.dma_start(out=lfs[:, :], in_=lf[:, sl]), w=["lfs"], dma=True)
        k.add("dve", lambda h: h.tensor_tensor_scan(out=cc[:, :], data0=onesr[:, :], data1=lfs[:, :], initial=carry[:, 0:1],
                                                    op0=ALU.mult, op1=ALU.add), r=["lfs", "onesr", "carry"], w=["cc"])
        k.add("dve", lambda h: h.tensor_copy(out=carry[:, :], in_=cc[:, CS - 1:CS]), r=["cc"], w=["carry"])
        k.add("dve", lambda h: h.tensor_copy(out=pcs[0][:, :], in_=cc[:, :]), r=["cc"], w=["pc0"])
        k.add("dve", lambda h: h.tensor_tensor(out=rr[:, :], in0=cc[:, :], in1=pcs[0][:, :], op=ALU.subtract), r=["cc", "pc0"], w=["rr"])
        k.add("dve", lambda h: h.tensor_copy(out=pcs[1][:, :], in_=rr[:, :]), r=["rr"], w=["pc1"])
        k.add("dve", lambda h: h.tensor_tensor(out=rr[:, :], in0=rr[:, :], in1=pcs[1][:, :], op=ALU.subtract), r=["rr", "pc1"], w=["rr"])
        k.add("dve", lambda h: h.tensor_copy(out=pcs[2][:, :], in_=rr[:, :]), r=["rr"], w=["pc2"])
        for i in range(3):
            k.add("dve", lambda h, i=i: h.tensor_scalar(out=ncs[i][:, :], in0=pcs[i][:, :], scalar1=-1.0, scalar2=None, op0=ALU.mult),
                  r=["pc%d" % i], w=["nc%d" % i])
            k.add("sp", lambda h, i=i, sl=sl: h.dma_start(out=caq[:, i, sl], in_=pcs[i][:, :]), r=["pc%d" % i], w=[("caq", i, j)], dma=True)
            k.add("sp", lambda h, i=i, sl=sl: h.dma_start(out=cak[:, i, sl], in_=ncs[i][:, :]), r=["nc%d" % i], w=[("cak", i, j)], dma=True)
    caqk = [("caq", i, j) for i in range(3) for j in range(S // CS)]
    cakk = [("cak", i, j) for i in range(3) for j in range(S // CS)]

    kaug = [k.sb("kaug%d" % i, [KA, S], BF16) for i in range(2)]
    vt = [k.sb("vt%d" % i, [128, NKB, HD + 1], BF16) for i in range(2)]
    qaug = [k.sb("qaug%d" % i, [KA, 512], BF16) for i in range(2)]
    for i in range(2):
        k.add("dve", lambda h, i=i: h.memset(kaug[i][64:KA, :], 1.0), w=[("kaug", i)])
        k.add("dve", lambda h, i=i: h.memset(vt[i][:, :, HD:HD + 1], 1.0), w=[("vt", i)])
        k.add("dve", lambda h, i=i: h.memset(qaug[i][64:KA, :], 1.0), w=[("qaug", i)])
    NPT = 4
    ptl = [k.sb("pT%d" % i, [128, 512], BF16) for i in range(NPT)]
    osb = [k.sb("osb%d" % i, [HD + 1, 512], F32) for i in range(2)]
    rd = [k.sb("rd%d" % i, [HD + 1, 512], F32) for i in range(2)]
    ot = [k.sb("ot%d" % i, [HD, 512], BF16) for i in range(2)]
    pti = 0
    qi = 0
    for hh in range(HPC):
        bi = hh % 2
        k.add("sp", lambda h, hh=hh, bi=bi: h.dma_start(out=kaug[bi][0:HD, :], in_=kTh[hh]), w=[("kaug", bi)], dma=True)
        k.add("sp", lambda h, hh=hh, bi=bi: h.dma_start(out=kaug[bi][HD + 3:KA, :], in_=cak[hh]), r=cakk, w=[("kaug", bi)], dma=True)
        k.add("sp", lambda h, hh=hh, bi=bi: h.dma_start(out=vt[bi][:, :, 0:HD], in_=vh[hh]), w=[("vt", bi)], dma=True)
        for qt in range(NQT):
            qb = qi % 2
            qi += 1
            qs = slice(qt * 512, (qt + 1) * 512)
            k.add("sp", lambda h, hh=hh, qb=qb, qs=qs: h.dma_start(out=qaug[qb][0:HD, :], in_=qTh[hh][:, qs]), w=[("qaug", qb)], dma=True)
            k.add("sp", lambda h, hh=hh, qb=qb, qs=qs: h.dma_start(out=qaug[qb][HD:HD + 3, :], in_=caq[hh][:, qs]), r=caqk, w=[("qaug", qb)], dma=True)
            po, pok = k.psum[6 + qb], ("ps", 6 + qb)
            nkb = (qt + 1) * 4
            LA = 2
            pend = []
            for step in range(nkb + LA):
                if step < nkb:
                    kb = step
                    d = kb - qt * 4
                    qo = 128 * d if d > 0 else 0
                    st_, stk = k.ps()

                    def smm(h, st_=st_, kb=kb, bi=bi, qb=qb, qo=qo, d=d):
                        ins = h.matmul(st_[:, qo:512], lhsT=kaug[bi][:, kb * 128:(kb + 1) * 128], rhs=qaug[qb][:, qo:512],
                                       start=True, stop=(d < 0))
                        if d >= 0:
                            ins = h.matmul(st_[:, qo:qo + 128], lhsT=ident[:, :], rhs=maskn[:, :], start=False, stop=True)
                        return ins
                    k.add("pe", smm, r=[("kaug", bi), ("qaug", qb), "ident", "maskn"], w=[stk])
                    pi = pti
                    pti = (pti + 1) % NPT
                    k.add("act", lambda h, st_=st_, pi=pi, qo=qo: h.activation(out=ptl[pi][:, qo:512], in_=st_[:, qo:512], func=AF.Exp),
                          r=[stk], w=[("pT", pi)])
                    pend.append((kb, pi, qo))
                if step >= LA:
                    kb, pi, qo = pend.pop(0)
                    k.add("pe", lambda h, po=po, bi=bi, kb=kb, pi=pi, qo=qo, nkb=nkb: h.matmul(
                        po[0:HD + 1, qo:512], lhsT=vt[bi][:, kb, :], rhs=ptl[pi][:, qo:512], start=(kb == 0), stop=(kb == nkb - 1)),
                        r=[("vt", bi), ("pT", pi)], w=[pok])
            k.add("act", lambda h, po=po, qb=qb: h.activation(out=osb[qb][:, :], in_=po[0:HD + 1, :], func=AF.Identity), r=[pok], w=[("osb", qb)])
            k.add("dve", lambda h, qb=qb: h.reciprocal(out=rd[qb][HD:HD + 1, :], in_=osb[qb][HD:HD + 1, :]), r=[("osb", qb)], w=[("rd", qb)])
            pb, pbk = k.ps()
            k.add("pe", lambda h, pb=pb, qb=qb: h.matmul(pb[0:HD, :], lhsT=onesf[HD:HD + 1, :], rhs=rd[qb][HD:HD + 1, :], start=True, stop=True),
                  r=["onesf", ("rd", qb)], w=[pbk])
            k.add("dve", lambda h, pb=pb, qb=qb: h.tensor_tensor(out=ot[qb][:, :], in0=osb[qb][0:HD, :], in1=pb[0:HD, :], op=ALU.mult),
                  r=[pbk, ("osb", qb)], w=[("ot", qb)])
            k.out_dmas.append(k.add("sp", lambda h, hh=hh, qb=qb, qs=qs: h.dma_start(out=oT_o[hh][:, qs], in_=ot[qb][:, :]),
                                    r=[("ot", qb)], dma=True))


def seg_POST(k, layer):
    nc = k.nc
    j = layer - 2
    hT_i = nc.dram_tensor("hT_in", [128, NCH, T], F32, kind="ExternalInput").ap()
    oT_i = nc.dram_tensor("oT_in", [128, NCH, T], BF16, kind="ExternalInput").ap()
    pT = nc.dram_tensor("pT", [128, 2, C], F32, kind="ExternalInput").ap()
    names = ["attn_w_o", "ffn_w1", "ffn_w2", "ple_w_gate", "ple_w_proj"] + (["attn_w_q"] if layer == 2 else [])
    W = decl_weights(k, names)
    if layer == 2:
        hT_o = nc.dram_tensor("hT_out", [128, NCH, T], F32, kind="ExternalOutput").ap()
        qT_o = nc.dram_tensor("qT_out", [128, NCH, T], BF16, kind="ExternalOutput").ap()
    else:
        yT_o = nc.dram_tensor("yT_out", [128, NCH, T], F32, kind="ExternalOutput").ap()
    setup_common(k)
    k.h = k.sb("h", [128, NCH, C], F32)
    NT4 = tiles_of(HL, C)
    k.tile_id = {c0: i + 1 for i, (c0, n) in enumerate(NT4)}
    k.hkeys = lambda c0: [("h", k.tile_id[c0], c) for c in range(NCH)]
    k.bigkeys = lambda c0: [("big", k.tile_id[c0], c) for c in range(NCH)]
    for (c0, n) in NT4:
        k.add("sp", lambda h, c0=c0, n=n: h.dma_start(out=k.h[:, :, c0:c0 + n], in_=hT_i[:, :, c0 - HL:c0 - HL + n]),
              w=k.hkeys(c0), dma=True)
    k.big = k.sb("big", [128, NCH, UPAD + C], BF16)
    k.pTb = [k.sb("pTb%d" % i, [128, 2, 512], BF16) for i in range(2)]
    k.pti = 0
    wo = WM(k, W["attn_w_o"][j], 0, D, 0, D)
    for (c0, n) in NT4:
        ti = k.tile_id[c0]
        ob, okk = k.tp()
        k.add("sp", lambda h, ob=ob, c0=c0, n=n: h.dma_start(out=ob[:, :, 0:n], in_=oT_i[:, :, c0 - HL:c0 - HL + n]), w=okk, dma=True)

        def ep(m, pt, pk, c0=c0, n=n, ti=ti):
            k.add("dve", lambda h: h.tensor_tensor(out=k.h[:, m, c0:c0 + n], in0=pt[:, 0:n], in1=k.h[:, m, c0:c0 + n], op=ALU.add),
                  r=[pk, ("h", ti, m)], w=[("h", ti, m)])
        linear_fm(k, wo, ob, okk, 0, n, ep)
    ffn_ple(k, layer, NT4, pT, W["ffn_w1"][layer], W["ffn_w2"][layer], W["ple_w_gate"][layer], W["ple_w_proj"][layer])
    if layer == 2:
        q_proj(k, NT4, VEC_MIX + 3, W["attn_w_q"][1], qT_o)
        for (c0, n) in NT4:
            k.out_dmas.append(k.add("sp", lambda h, c0=c0, n=n: h.dma_start(out=hT_o[:, :, c0 - HL:c0 - HL + n], in_=k.h[:, :, c0:c0 + n]),
                                    r=k.hkeys(c0), dma=True))
    else:
        for (c0, n) in NT4:
            rms_stats(k, k.h, k.hkeys(c0), c0, n)
            for c in range(NCH):
                y, yk = k.tf()
                k.add("dve", lambda h, y=y, c=c, c0=c0, n=n: h.scalar_tensor_tensor(
                    out=y[:, 0:n], in0=k.h[:, c, c0:c0 + n], scalar=vec(k, VEC_FIN, c), in1=k.rinv[:, 0:n], op0=ALU.mult, op1=ALU.mult),
                    r=[k.hkeys(c0)[c], "rinv", "vecs"], w=[yk])
                k.out_dmas.append(k.add("sp", lambda h, y=y, c=c, c0=c0, n=n: h.dma_start(out=yT_o[:, c, c0 - HL:c0 - HL + n], in_=y[:, 0:n]),
                                        r=[yk], dma=True))


def _fm(a2d):
    F = a2d.shape[1]
    return np.ascontiguousarray(a2d.T.reshape(F // 128, 128, a2d.shape[0]).transpose(1, 0, 2))


def _halo_slice(a, b, s0):
    if s0 >= HL:
        return a[b, s0 - HL:s0 + T]
    pad = np.zeros((HL,) + a.shape[2:], a.dtype)
    return np.concatenate([pad, a[b, 0:T]], axis=0)


def prep_common(inp):
    vl = [inp["mix_norm"][i] for i in range(4)] + [inp["ffn_norm"][i] for i in range(4)] + \
         [inp["ple_norm"][i] for i in range(4)] + [inp["kv_norm"], inp["final_norm"]]
    for i in range(2):
        vl += [inp["conv_b_pw1"][i][:D], inp["conv_b_pw1"][i][D:], inp["conv_b_dw"][i], inp["conv_ln_g"][i],
               inp["conv_ln_b"][i], inp["conv_b_pw2"][i]]
    vt = np.stack(vl, axis=0).astype(np.float32)
    vecs = np.ascontiguousarray(vt.reshape(NV, NCH, 128).transpose(2, 1, 0))
    wdw = np.ascontiguousarray(inp["conv_w_dw"].reshape(2, CW, NCH, 128).transpose(0, 3, 2, 1))
    return vecs, wdw


def core_pos(core):
    b = core // 4
    s0 = (core % 4) * T
    return b, s0


def in_maps_A(inp):
    vecs, wdw = prep_common(inp)
    maps = []
    for core in range(NCORES):
        b, s0 = core_pos(core)
        xT = _fm(_halo_slice(inp["x"], b, s0))
        pT = np.stack([_fm(_halo_slice(inp["p"][l], b, s0)) for l in range(2)], axis=0)
        hm = np.full((128, 1), 1.0 if s0 > 0 else 0.0, np.float32)
        maps.append({"xT": xT, "pT": pT, "hmask": hm, "wdw": wdw, "bf": inp["b_f"].reshape(NH, 1).astype(np.float32),
                     "vecs": vecs, "conv_w_pw1": inp["conv_w_pw1"], "conv_w_pw2": inp["conv_w_pw2"],
                     "ffn_w1": inp["ffn_w1"], "ffn_w2": inp["ffn_w2"], "ple_w_gate": inp["ple_w_gate"],
                     "ple_w_proj": inp["ple_w_proj"], "w_kvf": inp["w_kvf"], "attn_w_q": inp["attn_w_q"]})
    return maps


def _unfm(a):
    return a.transpose(1, 0, 2).reshape(a.shape[0] * a.shape[1], a.shape[2])


def exchange_to_heads(resA_list, key):
    out = []
    full = [np.concatenate([_unfm(np.asarray(resA_list[b * 4 + j][key])) for j in range(4)], axis=1) for b in range(B)]
    for core in range(NCORES):
        b, j = core // 4, core % 4
        out.append(np.ascontiguousarray(full[b][j * HPC * HD:(j + 1) * HPC * HD].reshape(HPC, HD, S)))
    return out


def exchange_v(resA_list):
    out = []
    full = [np.concatenate([np.asarray(resA_list[b * 4 + j]["v_out"]).transpose(1, 0, 2).reshape(T, D) for j in range(4)], axis=0)
            for b in range(B)]
    for core in range(NCORES):
        b, j = core // 4, core % 4
        v = full[b][:, j * HPC * HD:(j + 1) * HPC * HD].reshape(S // 128, 128, HPC, HD)
        out.append(np.ascontiguousarray(v.transpose(2, 1, 0, 3)))
    return out


def exchange_lf(resA_list):
    out = []
    full = [np.concatenate([np.asarray(resA_list[b * 4 + j]["lf_out"]) for j in range(4)], axis=1) for b in range(B)]
    for core in range(NCORES):
        b, j = core // 4, core % 4
        out.append(np.ascontiguousarray(full[b][j * HPC:(j + 1) * HPC]))
    return out


def exchange_o(resATT_list):
    out = []
    full = [np.concatenate([np.asarray(resATT_list[b * 4 + j]["oT_out"]).reshape(HPC * HD, S) for j in range(4)], axis=0) for b in range(B)]
    for core in range(NCORES):
        b, s0 = core_pos(core)
        o = full[b][:, s0:s0 + T]
        out.append(np.ascontiguousarray(o.reshape(NCH, 128, T).transpose(1, 0, 2)))
    return out


_PROGS = {}


def _prog(seg, layer=None):
    key = (seg, layer)
    if key not in _PROGS:
        _PROGS[key] = build_program(seg, layer)
    return _PROGS[key]


def _run(nc, maps):
    return run_bass_kernel_spmd(nc, maps, core_ids=list(range(NCORES))).results


def kernel(**inp):
    inp = {k_: np.asarray(v) for k_, v in inp.items()}
    vecs, _ = prep_common(inp)
    rA = _run(_prog("A"), in_maps_A(inp))
    kTh = exchange_to_heads(rA, "kT_out")
    vh = exchange_v(rA)
    lfh = exchange_lf(rA)
    hT = [np.asarray(r["hT_out"]) for r in rA]
    qsrc = rA
    y = None
    for layer in (2, 3):
        qTh = exchange_to_heads(qsrc, "qT_out")
        rT = _run(_prog("ATT", layer), [{"qTh": qTh[c], "kTh": kTh[c], "vh": vh[c], "lf": lfh[c]} for c in range(NCORES)])
        oT = exchange_o(rT)
        maps = []
        for c in range(NCORES):
            b, s0 = core_pos(c)
            m = {"hT_in": hT[c], "oT_in": oT[c], "pT": _fm(_halo_slice(inp["p"][layer], b, s0)), "vecs": vecs,
                 "attn_w_o": inp["attn_w_o"], "ffn_w1": inp["ffn_w1"], "ffn_w2": inp["ffn_w2"],
                 "ple_w_gate": inp["ple_w_gate"], "ple_w_proj": inp["ple_w_proj"]}
            if layer == 2:
                m["attn_w_q"] = inp["attn_w_q"]
            maps.append(m)
        rP = _run(_prog("POST", layer), maps)
        if layer == 2:
            hT = [np.asarray(r["hT_out"]) for r in rP]
            qsrc = rP
        else:
            y = rP
    out = np.empty((B, S, D), np.float32)
    for c in range(NCORES):
        b, s0 = core_pos(c)
        out[b, s0:s0 + T] = _unfm(np.asarray(y[c]["yT_out"])).T
    return out
```
